# Optimizing a Trainium2 kernel written in Bass

```python
import jax, jax.numpy as jnp
from jax import lax
import numpy as np

D_MODEL = 1024
BATCH = 2
SEQ = 8192
DEPTH = 2

CHUNK = 64
CONV_WIDTH = D_MODEL
CONV_GROUPS = 16
CONV_K = 3
LRU_WIDTH = D_MODEL
LRU_HEADS = 16
LRU_HEAD_DIM = LRU_WIDTH // LRU_HEADS
LRU_CONV_K = 4
LRU_C = 8.0
D_FF = ((-(-8 * D_MODEL // 3)) + 255) // 256 * 256
RMS_EPS = 1e-6
IN_WIDTHS = (CONV_WIDTH, CONV_WIDTH, CONV_WIDTH, LRU_WIDTH, LRU_WIDTH, D_MODEL, D_MODEL)
IN_TOTAL = sum(IN_WIDTHS)
SPLIT_POINTS = tuple(int(v) for v in np.cumsum(IN_WIDTHS)[:-1])

kernel_name = "hybrid_shortconv_rglru_gated_merge"


def rmsnorm(x, g):
    xf = x.astype(jnp.float32)
    var = jnp.mean(xf * xf, axis=-1, keepdims=True)
    return (xf * lax.rsqrt(var + RMS_EPS) * g.astype(jnp.float32)).astype(x.dtype)


def causal_depthwise_conv(x, w, b=None):
    K = w.shape[0]
    S = x.shape[1]
    xp = jnp.pad(x, ((0, 0), (K - 1, 0), (0, 0)))
    y = xp[:, 0:S] * w[0]
    for k in range(1, K):
        y = y + xp[:, k:k + S] * w[k]
    if b is not None:
        y = y + b
    return y


def rg_lru(x, w_a, b_a, w_x, b_x, lam):
    Bsz, S, W = x.shape
    f32 = jnp.float32
    xf = x.astype(f32)
    xh = xf.reshape(Bsz, S, LRU_HEADS, LRU_HEAD_DIM)
    r = jax.nn.sigmoid(jnp.einsum('bshd,hde->bshe', xh, w_a.astype(f32)).reshape(Bsz, S, W) + b_a.astype(f32))
    i = jax.nn.sigmoid(jnp.einsum('bshd,hde->bshe', xh, w_x.astype(f32)).reshape(Bsz, S, W) + b_x.astype(f32))
    log_a = -LRU_C * r * jax.nn.softplus(-lam.astype(f32))
    a = jnp.exp(log_a)
    b = jnp.sqrt(-jnp.expm1(2.0 * log_a)) * (i * xf)
    n_chunks = S // CHUNK
    a_c = a.reshape(Bsz, n_chunks, CHUNK, W)
    b_c = b.reshape(Bsz, n_chunks, CHUNK, W)

    def combine(left, right):
        a_l, b_l = left
        a_r, b_r = right
        return a_l * a_r, a_r * b_l + b_r

    a_cum, h_loc = lax.associative_scan(combine, (a_c, b_c), axis=2)

    def step(h_prev, inp):
        a_cum_k, h_loc_k = inp
        h = h_loc_k + a_cum_k * h_prev[:, None, :]
        return h[:, -1], h

    h0 = jnp.zeros((Bsz, W), f32)
    _, hs = lax.scan(step, h0, (jnp.moveaxis(a_cum, 1, 0), jnp.moveaxis(h_loc, 1, 0)))
    return jnp.moveaxis(hs, 0, 1).reshape(Bsz, S, W).astype(x.dtype)


def setup_inputs(seed: int = 0) -> dict:
    key = jax.random.key(seed)
    ks = jax.random.split(key, 24)
    f32 = jnp.float32
    nrm = lambda k, shape, fan_in: jax.random.normal(k, shape, f32) * (fan_in ** -0.5)
    x = jax.random.normal(ks[0], (BATCH, SEQ, D_MODEL), f32)
    ln1_g = 1.0 + 0.02 * jax.random.normal(ks[1], (DEPTH, D_MODEL), f32)
    w_in = nrm(ks[2], (DEPTH, D_MODEL, IN_TOTAL), D_MODEL)
    conv_a_w = nrm(ks[3], (DEPTH, CONV_K, CONV_WIDTH), CONV_K)
    conv_b_w = nrm(ks[4], (DEPTH, LRU_CONV_K, LRU_WIDTH), LRU_CONV_K)
    conv_b_b = 0.02 * jax.random.normal(ks[5], (DEPTH, LRU_WIDTH), f32)
    lru_wa = nrm(ks[6], (DEPTH, LRU_HEADS, LRU_HEAD_DIM, LRU_HEAD_DIM), LRU_HEAD_DIM)
    lru_ba = 0.02 * jax.random.normal(ks[7], (DEPTH, LRU_WIDTH), f32)
    lru_wx = nrm(ks[8], (DEPTH, LRU_HEADS, LRU_HEAD_DIM, LRU_HEAD_DIM), LRU_HEAD_DIM)
    lru_bx = 0.02 * jax.random.normal(ks[9], (DEPTH, LRU_WIDTH), f32)
    u = jax.random.uniform(ks[10], (DEPTH, LRU_WIDTH), f32, 0.9, 0.999)
    s = u ** (1.0 / LRU_C)
    lru_lambda = jnp.log(s) - jnp.log1p(-s)
    w_out_a = nrm(ks[11], (DEPTH, CONV_WIDTH, D_MODEL), CONV_WIDTH)
    w_out_b = nrm(ks[12], (DEPTH, LRU_WIDTH, D_MODEL), LRU_WIDTH)
    gate_bias = 0.02 * jax.random.normal(ks[13], (DEPTH, 2, D_MODEL), f32)
    w_o = nrm(ks[14], (DEPTH, D_MODEL, D_MODEL), D_MODEL)
    ln2_g = 1.0 + 0.02 * jax.random.normal(ks[15], (DEPTH, D_MODEL), f32)
    w_ffn_gate = nrm(ks[16], (DEPTH, D_MODEL, D_FF), D_MODEL)
    w_ffn_up = nrm(ks[17], (DEPTH, D_MODEL, D_FF), D_MODEL)
    w_ffn_down = nrm(ks[18], (DEPTH, D_FF, D_MODEL), D_FF)
    final_g = 1.0 + 0.02 * jax.random.normal(ks[19], (D_MODEL,), f32)
    return {"x": x, "ln1_g": ln1_g, "w_in": w_in, "conv_a_w": conv_a_w,
            "conv_b_w": conv_b_w, "conv_b_b": conv_b_b, "lru_wa": lru_wa,
            "lru_ba": lru_ba, "lru_wx": lru_wx, "lru_bx": lru_bx,
            "lru_lambda": lru_lambda, "w_out_a": w_out_a, "w_out_b": w_out_b,
            "gate_bias": gate_bias, "w_o": w_o, "ln2_g": ln2_g,
            "w_ffn_gate": w_ffn_gate, "w_ffn_up": w_ffn_up,
            "w_ffn_down": w_ffn_down, "final_g": final_g}


def reference(x, ln1_g, w_in, conv_a_w, conv_b_w, conv_b_b, lru_wa, lru_ba,
              lru_wx, lru_bx, lru_lambda, w_out_a, w_out_b, gate_bias, w_o,
              ln2_g, w_ffn_gate, w_ffn_up, w_ffn_down, final_g):
    for l in range(DEPTH):
        h = rmsnorm(x, ln1_g[l])
        proj = h @ w_in[l]
        b_a, c_a, x_a, x_b, g_b, gate_a_logit, gate_b_logit = jnp.split(proj, SPLIT_POINTS, axis=-1)
        y_a = b_a * causal_depthwise_conv(c_a * x_a, conv_a_w[l])
        u_b = causal_depthwise_conv(x_b, conv_b_w[l], conv_b_b[l])
        y_b = rg_lru(u_b, lru_wa[l], lru_ba[l], lru_wx[l], lru_bx[l], lru_lambda[l])
        y_b = y_b * jax.nn.gelu(g_b)
        merged = (jax.nn.sigmoid(gate_a_logit + gate_bias[l, 0]) * (y_a @ w_out_a[l])
                  + jax.nn.sigmoid(gate_b_logit + gate_bias[l, 1]) * (y_b @ w_out_b[l]))
        x = x + merged @ w_o[l]
        h = rmsnorm(x, ln2_g[l])
        x = x + (jax.nn.silu(h @ w_ffn_gate[l]) * (h @ w_ffn_up[l])) @ w_ffn_down[l]
    return rmsnorm(x, final_g)
```

```python
from contextlib import ExitStack
import numpy as np
import concourse.bass as bass
import concourse.mybir as mybir
from concourse.bass_utils import run_bass_kernel_spmd

F32 = mybir.dt.float32
BF16 = mybir.dt.bfloat16
AF = mybir.ActivationFunctionType
ALU = mybir.AluOpType

FUSED = False
NCORE = 8
L = 2
D = 1024
KC = 8
DFF = 2816
FC = 22
NTOK = 2048
T = 512
NT = NTOK // T
EPS = 1e-6
NV = 120
V_LN1, V_CAW, V_CBW, V_CBB, V_BA, V_BX, V_LAM, V_GBA, V_GBB, V_LN2 = 0, 8, 32, 64, 72, 80, 88, 96, 104, 112
EXW = 32
SLOT = 4096
NSLOT = 4
ROWW = 2048

COMPUTE = ("pe", "act", "dve", "pool")
ALLENG = ("pe", "act", "dve", "pool", "sp")


def slab_table():
    tab = []
    for g in range(14):
        tab.append((("IP", g), ("w_in", 0, 8, g * 512, 512)))
    for nm, key in (("OA", "w_out_a"), ("OB", "w_out_b"), ("WO", "w_o")):
        for g in range(2):
            tab.append(((nm, g), (key, 0, 8, g * 512, 512)))
    for nm, key in (("FG", "w_ffn_gate"), ("FU", "w_ffn_up")):
        for g in range(6):
            nc_ = 512 if g < 5 else 256
            tab.append(((nm, g), (key, 0, 8, g * 512, nc_)))
    for cg in range(2):
        for kg in range(3):
            nk = 8 if kg < 2 else 6
            tab.append((("FD", cg, kg), ("w_ffn_down", kg * 8, nk, cg * 512, 512)))
    tab.append((("LRU",), ("lru", 0, 16, 0, 128)))
    offs = {}
    off = 0
    for name, (key, k0, nk, c0, nc_) in tab:
        offs[name] = (off, nk, nc_)
        off += 128 * nk * nc_
    assert off % ROWW == 0
    return tab, offs, off


SLABS, SLAB_OFF, WTOT = slab_table()
WROWS = WTOT // ROWW


class Buf:
    __slots__ = ("lw", "rd")

    def __init__(self):
        self.lw = None
        self.rd = {}


def bufs(n):
    return [Buf() for _ in range(n)]


class Prog:
    def __init__(self, nc, stack):
        self.nc = nc
        self.stack = stack
        self.streams = {e: [] for e in ALLENG}
        self.sems = {}
        self.cnt = {}
        self.seen = {e: {} for e in ALLENG}
        self.planning = False
        for e in COMPUTE:
            self.newsem(e)

    def newsem(self, key):
        self.sems[key] = self.stack.enter_context(self.nc.semaphore("s_" + str(key)))
        self.cnt[key] = 0
        return key

    def sbuf(self, name, shape, dt):
        return self.stack.enter_context(self.nc.sbuf_tensor(name, list(shape), dt))

    def psum(self, name, shape, dt):
        return self.stack.enter_context(self.nc.psum_tensor(name, list(shape), dt))

    def _deps(self, eng, reads, writes):
        need = {}
        own = eng if eng in COMPUTE else None

        def add(ev, raw):
            if ev is None:
                return
            k, v = ev
            if k == own and not raw:
                return
            if need.get(k, 0) < v:
                need[k] = v

        for r in reads:
            add(r.lw, True)
        for w in writes:
            add(w.lw, False)
            for k, v in w.rd.items():
                add((k, v), False)
        out = []
        for k, v in need.items():
            if self.seen[eng].get(k, 0) < v:
                self.seen[eng][k] = v
                out.append((k, v))
        return out

    def _emit_waits(self, eng, waits):
        for k, v in waits:
            sem = self.sems[k]
            self.streams[eng].append(lambda e, sem=sem, v=v: e.wait_ge(sem, v))

    def _mark(self, ev, reads, writes):
        k, v = ev
        for w in writes:
            w.lw = ev
            w.rd = {}
        for r in reads:
            if r.rd.get(k, 0) < v:
                r.rd[k] = v

    def op(self, eng, fn, reads=(), writes=()):
        if self.planning:
            return None
        self._emit_waits(eng, self._deps(eng, reads, writes))
        self.cnt[eng] += 1
        ev = (eng, self.cnt[eng])
        sem = self.sems[eng]
        self.streams[eng].append(lambda e, fn=fn, sem=sem: fn(e).then_inc(sem, 1))
        self._mark(ev, reads, writes)
        return ev

    def dma(self, q, out_ap, in_ap, reads=(), writes=(), semkey=None, **kw):
        if self.planning:
            return None
        self._emit_waits(q, self._deps(q, reads, writes))
        self.cnt[semkey] += 16
        ev = (semkey, self.cnt[semkey])
        sem = self.sems[semkey]
        self.streams[q].append(
            lambda e, o=out_ap, i=in_ap, sem=sem, kw=kw: e.dma_start(out=o, in_=i, **kw).then_inc(sem, 16))
        self._mark(ev, reads, writes)
        return ev

    def raw(self, eng, fn):
        if not self.planning:
            self.streams[eng].append(fn)

    def finish(self):
        nc = self.nc
        S = self.streams
        with nc.Block() as block:
            @block.sync
            def _(e):
                for f in S["sp"]:
                    f(e)

            @block.tensor
            def _(e):
                for f in S["pe"]:
                    f(e)

            @block.scalar
            def _(e):
                for f in S["act"]:
                    f(e)

            @block.vector
            def _(e):
                for f in S["dve"]:
                    f(e)

            @block.gpsimd
            def _(e):
                for f in S["pool"]:
                    f(e)


class Ring:
    def __init__(self, aps):
        self.aps = aps
        self.bs = bufs(len(aps))
        self.i = 0

    def get(self):
        k = self.i % len(self.aps)
        self.i += 1
        return self.aps[k], self.bs[k]


class Kern:
    def __init__(self, stop, fused):
        self.stop = stop
        self.fused = fused
        self.done = False

    def build(self):
        nc = bass.Bass("TRN2", target_bir_lowering=False)
        self.nc = nc
        dt = nc.dram_tensor
        self.x_in = dt("x_in", [128, KC, NTOK], F32, kind="ExternalInput").ap()
        self.xh_in = dt("xh_in", [128, KC, 4], F32, kind="ExternalInput").ap()
        self.vec_in = dt("vec_in", [128, L * NV + 8], F32, kind="ExternalInput").ap()
        self.msk_in = dt("msk_in", [128, 16], F32, kind="ExternalInput").ap()
        self.w_in = dt("w_in", [L, WROWS, ROWW], F32, kind="ExternalInput").ap()
        self.wsc = dt("wsc", [L, WROWS, ROWW], BF16, kind="Internal").ap()
        nex = 3
        self.exg_in = []
        if not self.fused:
            for i in range(self.stop):
                self.exg_in.append(dt(f"exg{i}", [128, NCORE, EXW], F32, kind="ExternalInput").ap())
            if self.stop < nex:
                self.exo = dt("exo", [128, EXW], F32, kind="ExternalOutput").ap()
        else:
            self.exl = [dt(f"exl{i}", [128, EXW], F32, kind="Internal").ap() for i in range(nex)]
            self.exgd = [dt(f"exgd{i}", [NCORE * 128, EXW], F32, kind="Internal").ap() for i in range(nex)]
        if self.fused or self.stop == nex:
            self.y_out = dt("y_out", [128, KC, NTOK], F32, kind="ExternalOutput").ap()
        with ExitStack() as st:
            P = Prog(nc, st)
            self.P = P
            self.alloc()
            P.planning = True
            self.plan = []
            self.done = False
            self.program()
            P.planning = False
            self.done = False
            self.reset_state()
            self.program()
            self.finalize_wait()
            P.finish()
        return nc

    def alloc(self):
        P = self.P
        sb = P.sbuf
        self.xs = sb("xs", [128, KC, NTOK], F32)
        self.Bxs = [bufs(NT) for _ in range(KC)]
        self.xh = sb("xh", [128, KC, 4], F32)
        self.Bxh = Buf()
        self.vec = sb("vec", [128, L * NV + 8], F32)
        self.Bvec = Buf()
        self.der = sb("der", [128, L * 16], F32)
        self.Bder = Buf()
        self.msk = sb("msk", [128, 16], F32)
        self.Bmsk = Buf()
        self.lru = sb("lru", [128, L, 16, 128], BF16)
        self.Blru = bufs(L)
        self.ones = sb("ones", [128, 128], BF16)
        self.Bones = Buf()
        self.h = sb("h", [128, KC, T], BF16)
        self.Bh = bufs(KC)
        self.h4 = sb("h4", [128, KC, 4], BF16)
        self.Bh4 = Buf()
        self.sq4 = sb("sq4", [128, KC, 4], BF16)
        self.Bsq4 = Buf()
        self.r24 = sb("r24", [128, 24, T], BF16)
        self.B24 = bufs(24)
        self.rstd = sb("rstd", [128, T], F32)
        self.Brstd = Buf()
        self.rstd4 = sb("rstd4", [128, 4], F32)
        self.Brstd4 = Buf()
        xbr = sb("xbr", [128, 4, 516], F32)
        self.xbring = Ring([xbr[:, i, :] for i in range(4)])
        zr = sb("zr", [128, 2, 516], F32)
        self.zring = Ring([zr[:, i, :] for i in range(2)])
        self.xbh = sb("xbh", [128, KC, 3], F32)
        self.Bxbh = bufs(KC)
        self.xbh0 = sb("xbh0", [128, KC, 3], F32)
        self.Bxbh0 = bufs(KC)
        self.zh = sb("zh", [128, KC, 2], F32)
        self.Bzh = bufs(KC)
        self.zh0 = sb("zh0", [128, KC, 2], F32)
        self.Bzh0 = bufs(KC)
        self.ca4 = sb("ca4", [128, KC, 4], F32)
        self.Bca4 = bufs(KC)
        ub = sb("ubr", [128, 4, T], BF16)
        self.ubring = Ring([ub[:, i, :] for i in range(4)])
        ur = sb("ur", [128, 4, T], F32)
        self.uring = Ring([ur[:, i, :] for i in range(4)])
        for nm in ("r", "i", "a", "s", "t"):
            tt = sb("tr_" + nm, [128, 2, T], F32)
            setattr(self, nm + "ring", Ring([tt[:, i, :] for i in range(2)]))
        g4 = sb("g4", [128, 2, 4, T], F32)
        self.g4 = [[g4[:, i, j, :] for j in range(4)] for i in range(2)]
        self.Bg4 = [bufs(4) for _ in range(2)]
        self.hst = sb("hst", [128, KC], F32)
        self.Bhst = bufs(KC)
        self.rs = sb("rs", [128, KC, NT], F32)
        self.Brs = Buf()
        self.car = sb("car", [128, EXW], F32)
        self.Bcar = Buf()
        self.exg = sb("exg", [128, NCORE, EXW], F32)
        self.Bexg = Buf()
        self.fold = sb("fold", [128, 3, KC], F32)
        self.Bfold = Buf()
        slots = sb("wslots", [128, NSLOT, SLOT], BF16)
        self.slots = [slots[:, i, :] for i in range(NSLOT)]
        self.Bslot = bufs(NSLOT)
        self.slotsem = [P.newsem(("slot", i)) for i in range(NSLOT)]
        ps = [P.psum(f"ps{i}", [128, T], F32) for i in range(8)]
        self.psring = Ring([p[:] for p in ps])
        self.cvsem = [P.newsem(("cv", i)) for i in range(16)]
        self.Bcv = {}
        for k in ("ld", "st", "ex"):
            P.newsem(k)
        self.Bexd = [Buf() for _ in range(3)]

    def reset_state(self):
        for nm in ("xbring", "zring", "ubring", "uring", "rring", "iring", "aring", "sring", "tring", "psring"):
            getattr(self, nm).i = 0
        self.plan_pos = 0
        self.slot_of = {}
        self.slot_free = list(range(NSLOT))
        self.next_load = 0
        self.ncv = 0

    def convert_layer(self, l):
        P = self.P
        if P.planning:
            return
        for name, _ in SLABS:
            off, nk, nc_ = SLAB_OFF[name]
            r0 = off // ROWW
            nr = 128 * nk * nc_ // ROWW
            k = self.ncv
            self.ncv += 1
            sk = self.cvsem[k % 16]
            prev = P.cnt[sk]
            if prev:
                sem = P.sems[sk]
                P.streams["pool"].append(lambda e, sem=sem, v=prev: e.wait_ge(sem, v))
            b = Buf()
            self.Bcv[(l, name)] = b
            P.dma("pool", self.wsc[l, r0:r0 + nr, :], self.w_in[l, r0:r0 + nr, :], writes=[b], semkey=sk)

    def _issue_load(self):
        P = self.P
        while self.next_load < len(self.plan) and self.slot_free:
            l, name = self.plan[self.next_load]
            self.next_load += 1
            s = self.slot_free.pop(0)
            off, nk, nc_ = SLAB_OFF[name]
            n = nk * nc_
            src = self.wsc[l].rearrange("r w -> (r w)")[off:off + 128 * n].rearrange("(p n) -> p n", p=128)
            P.dma("sp", self.slots[s][:, 0:n], src, reads=[self.Bcv[(l, name)]], writes=[self.Bslot[s]],
                  semkey=self.slotsem[s])
            self.slot_of[(self.next_load - 1)] = s

    def need(self, l, name):
        if self.P.planning:
            self.plan.append((l, name))
            return None, None
        idx = self.plan_pos
        assert self.plan[idx] == (l, name), (self.plan[idx], l, name)
        self.plan_pos += 1
        if idx not in self.slot_of:
            self._issue_load()
        s = self.slot_of[idx]
        off, nk, nc_ = SLAB_OFF[name]
        ap = self.slots[s][:, 0:nk * nc_].rearrange("p (k c) -> p k c", k=nk)
        self.cur_slot = s
        return ap, self.Bslot[s]

    def release(self):
        if self.P.planning:
            return
        self.slot_free.append(self.cur_slot)
        self._issue_load()

    def V(self, l, base, c):
        col = l * NV + base + c
        return self.vec[:, col:col + 1]

    def mm(self, ps, Bps, lhs, rhs, reads):
        def fn(e):
            n = len(lhs)
            ins = None
            for i in range(n):
                ins = e.matmul(ps, lhsT=lhs[i], rhs=rhs[i], start=(i == 0), stop=(i == n - 1))
            return ins
        self.P.op("pe", fn, reads, [Bps])

    def act(self, out, in_, func, reads, writes, **kw):
        self.P.op("act", lambda e: e.activation(out=out, in_=in_, func=func, **kw), reads, writes)

    def tt(self, out, in0, in1, op, reads, writes, eng="dve"):
        self.P.op(eng, lambda e: e.tensor_tensor(out=out, in0=in0, in1=in1, op=op), reads, writes)

    def stt(self, out, in0, scalar, in1, op0, op1, reads, writes):
        self.P.op("dve", lambda e: e.scalar_tensor_tensor(out=out, in0=in0, scalar=scalar, in1=in1, op0=op0, op1=op1),
                  reads, writes)

    def cp(self, out, in_, reads, writes, eng="dve"):
        self.P.op(eng, lambda e: e.tensor_copy(out=out, in_=in_), reads, writes)

    def norm(self, l, gbase, j, halo=False):
        sl = slice(j * T, (j + 1) * T)
        sqb = [self.B24[16 + c] for c in range(KC)]
        for c in range(KC):
            self.act(self.r24[:, 16 + c, :], self.xs[:, c, sl], AF.Square, [self.Bxs[c][j]], [sqb[c]])
        ps, Bps = self.psring.get()
        self.mm(ps, Bps, [self.ones[:]] * KC, [self.r24[:, 16 + c, :] for c in range(KC)], [self.Bones] + sqb)
        self.act(self.rstd[:], ps, AF.Sqrt, [Bps], [self.Brstd], scale=1.0 / D, bias=EPS)
        self.P.op("dve", lambda e: e.reciprocal(out=self.rstd[:], in_=self.rstd[:]), [self.Brstd], [self.Brstd])
        for c in range(KC):
            self.stt(self.h[:, c, :], self.xs[:, c, sl], self.vcol(l, gbase, c), self.rstd[:], ALU.mult, ALU.mult,
                     [self.Bxs[c][j], self.Brstd, self.Bvec], [self.Bh[c]])
        if halo:
            self.act(self.sq4[:], self.xh[:], AF.Square, [self.Bxh], [self.Bsq4])
            ps, Bps = self.psring.get()
            self.mm(ps[:, 0:4], Bps, [self.ones[:]] * KC, [self.sq4[:, c, :] for c in range(KC)],
                    [self.Bones, self.Bsq4])
            self.act(self.rstd4[:], ps[:, 0:4], AF.Sqrt, [Bps], [self.Brstd4], scale=1.0 / D, bias=EPS)
            self.P.op("dve", lambda e: e.reciprocal(out=self.rstd4[:], in_=self.rstd4[:]), [self.Brstd4], [self.Brstd4])
            for c in range(KC):
                self.stt(self.h4[:, c, :], self.xh[:, c, :], self.vcol(l, gbase, c), self.rstd4[:], ALU.mult, ALU.mult,
                         [self.Bxh, self.Brstd4, self.Bvec], [self.Bh4])

    def vcol(self, l, base, c):
        if base == "final":
            col = L * NV + c
            return self.vec[:, col:col + 1]
        return self.V(l, base, c)

    def proj(self, W, Bw, q, rhs_h=True, n4=False):
        ps, Bps = self.psring.get()
        nk = KC
        if n4:
            self.mm(ps[:, 0:4], Bps, [W[:, k, q * 128:(q + 1) * 128] for k in range(nk)],
                    [self.h4[:, k, :] for k in range(nk)], [Bw, self.Bh4])
            return ps[:, 0:4], Bps
        self.mm(ps, Bps, [W[:, k, q * 128:(q + 1) * 128] for k in range(nk)],
                [self.h[:, k, :] for k in range(nk)], [Bw] + self.Bh)
        return ps, Bps

    def mixer_b(self, l, j, sweep):
        P = self.P
        for half in range(2):
            W, Bw = self.need(l, ("IP", 6 + half))
            part1 = []
            for q in range(4):
                c = half * 4 + q
                if P.planning:
                    continue
                xb, Bxb = self.xbring.get()
                if j == 0 and sweep == 1:
                    p4, Bp4 = self.proj(W, Bw, q, n4=True)
                    self.act(self.xbh0[:, c, :], p4[:, 1:4], AF.Copy, [Bp4], [self.Bxbh0[c]])
                ps, Bps = self.proj(W, Bw, q)
                hsrc, Bhs = (self.xbh0, self.Bxbh0) if j == 0 else (self.xbh, self.Bxbh)
                self.cp(xb[:, 1:4], hsrc[:, c, :], [Bhs[c]], [Bxb])
                self.act(xb[:, 4:516], ps, AF.Copy, [Bps], [Bxb])
                u, Bu = self.uring.get()
                self.act(u, ps, AF.Identity, [Bps, self.Bvec], [Bu],
                         scale=self.V(l, V_CBW + 3 * 8, c), bias=self.V(l, V_CBB, c))
                for k in (2, 1, 0):
                    self.stt(u, xb[:, 1 + k:1 + k + T], self.V(l, V_CBW + k * 8, c), u, ALU.mult, ALU.add,
                             [Bxb, Bu, self.Bvec], [Bu])
                self.cp(self.xbh[:, c, :], xb[:, 513:516], [Bxb], [self.Bxbh[c]])
                ub, Bub = self.ubring.get()
                self.act(ub, u, AF.Copy, [Bu], [Bub])
                part1.append((c, u, Bu, ub, Bub))
            self.release()
            if P.planning:
                part1 = [(half * 4 + q, None, None, None, None) for q in range(4)]
            hh_list = []
            for (c, u, Bu, ub, Bub) in part1:
                if P.planning:
                    continue
                q = c % 4
                pr, Bpr = self.psring.get()
                self.mm(pr, Bpr, [self.lru[:, l, c, :]], [ub], [self.Blru[l], Bub])
                pi, Bpi = self.psring.get()
                self.mm(pi, Bpi, [self.lru[:, l, 8 + c, :]], [ub], [self.Blru[l], Bub])
                r, Br = self.rring.get()
                if sweep == 1:
                    P.op("act", lambda e, r=r, pr=pr, c=c: e.activation(
                        out=r, in_=pr, func=AF.Sigmoid, bias=self.V(l, V_BA, c), accum_out=self.rs[:, c, j:j + 1]),
                        [Bpr, self.Bvec], [Br, self.Brs])
                else:
                    self.act(r, pr, AF.Sigmoid, [Bpr, self.Bvec], [Br], bias=self.V(l, V_BA, c))
                i_, Bi = self.iring.get()
                self.act(i_, pi, AF.Sigmoid, [Bpi, self.Bvec], [Bi], bias=self.V(l, V_BX, c))
                a, Ba = self.aring.get()
                self.act(a, r, AF.Exp, [Br, self.Bder], [Ba], scale=self.der[:, l * 16 + c:l * 16 + c + 1])
                s, Bs = self.sring.get()
                self.act(s, r, AF.Exp, [Br, self.Bder], [Bs], scale=self.der[:, l * 16 + 8 + c:l * 16 + 9 + c])
                self.act(s, s, AF.Sqrt, [Bs], [Bs], scale=-1.0, bias=1.0)
                self.tt(i_, i_, u, ALU.mult, [Bi, Bu], [Bi])
                self.tt(s, s, i_, ALU.mult, [Bs, Bi], [Bs])
                self.stt(s[:, 0:1], a[:, 0:1], self.hst[:, c:c + 1], s[:, 0:1], ALU.mult, ALU.add,
                         [Ba, Bs, self.Bhst[c]], [Bs])
                hh, Bhh = self.g4[half][q], self.Bg4[half][q]
                P.op("dve", lambda e, hh=hh, a=a, s=s: e.tensor_tensor_scan(
                    out=hh, data0=a, data1=s, initial=0.0, op0=ALU.mult, op1=ALU.add), [Ba, Bs], [Bhh])
                self.cp(self.hst[:, c:c + 1], hh[:, T - 1:T], [Bhh], [self.Bhst[c]])
                hh_list.append((c, hh, Bhh))
            if sweep == 2:
                W, Bw = self.need(l, ("IP", 8 + half))
                for (c, hh, Bhh) in hh_list:
                    q = c % 4
                    ps, Bps = self.proj(W, Bw, q)
                    g, Bg = self.tring.get()
                    self.act(g, ps, AF.Gelu, [Bps], [Bg])
                    self.tt(self.r24[:, 8 + c, :], hh, g, ALU.mult, [Bhh, Bg], [self.B24[8 + c]])
                self.release()

    def mixer_a(self, l, j):
        P = self.P
        for half in range(2):
            G, BG = self.g4[half], self.Bg4[half]
            W, Bw = self.need(l, ("IP", 2 + half))
            for q in range(4):
                if P.planning:
                    continue
                c = half * 4 + q
                if j == 0:
                    p4, Bp4 = self.proj(W, Bw, q, n4=True)
                    self.act(self.ca4[:, c, :], p4, AF.Copy, [Bp4], [self.Bca4[c]])
                ps, Bps = self.proj(W, Bw, q)
                self.act(G[q], ps, AF.Copy, [Bps], [BG[q]])
            self.release()
            W, Bw = self.need(l, ("IP", 4 + half))
            for q in range(4):
                if P.planning:
                    continue
                c = half * 4 + q
                z, Bz = self.zring.get()
                if j == 0:
                    p4, Bp4 = self.proj(W, Bw, q, n4=True)
                    self.tt(self.zh0[:, c, :], self.ca4[:, c, 2:4], p4[:, 2:4], ALU.mult, [self.Bca4[c], Bp4],
                            [self.Bzh0[c]])
                ps, Bps = self.proj(W, Bw, q)
                hsrc, Bhs = (self.zh0, self.Bzh0) if j == 0 else (self.zh, self.Bzh)
                self.cp(z[:, 2:4], hsrc[:, c, :], [Bhs[c]], [Bz])
                self.tt(z[:, 4:516], G[q], ps, ALU.mult, [BG[q], Bps], [Bz])
                self.act(G[q], z[:, 4:516], AF.Copy, [Bz, self.Bvec], [BG[q]], scale=self.V(l, V_CAW + 2 * 8, c))
                for k in (1, 0):
                    self.stt(G[q], z[:, 2 + k:2 + k + T], self.V(l, V_CAW + k * 8, c), G[q], ALU.mult, ALU.add,
                             [Bz, BG[q], self.Bvec], [BG[q]])
                self.cp(self.zh[:, c, :], z[:, 514:516], [Bz], [self.Bzh[c]])
            self.release()
            W, Bw = self.need(l, ("IP", 0 + half))
            for q in range(4):
                if P.planning:
                    continue
                c = half * 4 + q
                ps, Bps = self.proj(W, Bw, q)
                self.tt(self.r24[:, c, :], G[q], ps, ALU.mult, [BG[q], Bps], [self.B24[c]])
            self.release()

    def merge(self, l, j):
        P = self.P
        for half in range(2):
            GA, BGA = self.g4[0], self.Bg4[0]
            GB, BGB = self.g4[1], self.Bg4[1]
            for (ipg, G, BG, base, vb) in ((10, GA, BGA, None, V_GBA), (12, GB, BGB, None, V_GBB)):
                W, Bw = self.need(l, ("IP", ipg + half))
                for q in range(4):
                    if P.planning:
                        continue
                    m = half * 4 + q
                    ps, Bps = self.proj(W, Bw, q)
                    self.act(G[q], ps, AF.Sigmoid, [Bps, self.Bvec], [BG[q]], bias=self.V(l, vb, m))
                self.release()
            W, Bw = self.need(l, ("OA", half))
            for q in range(4):
                if P.planning:
                    continue
                ps, Bps = self.psring.get()
                self.mm(ps, Bps, [W[:, k, q * 128:(q + 1) * 128] for k in range(KC)],
                        [self.r24[:, k, :] for k in range(KC)], [Bw] + self.B24[0:8])
                self.tt(GA[q], GA[q], ps, ALU.mult, [BGA[q], Bps], [BGA[q]])
            self.release()
            W, Bw = self.need(l, ("OB", half))
            for q in range(4):
                if P.planning:
                    continue
                m = half * 4 + q
                ps, Bps = self.psring.get()
                self.mm(ps, Bps, [W[:, k, q * 128:(q + 1) * 128] for k in range(KC)],
                        [self.r24[:, 8 + k, :] for k in range(KC)], [Bw] + self.B24[8:16])
                self.tt(GB[q], GB[q], ps, ALU.mult, [BGB[q], Bps], [BGB[q]])
                self.tt(self.r24[:, 16 + m, :], GB[q], GA[q], ALU.add, [BGB[q], BGA[q]], [self.B24[16 + m]])
            self.release()

    def wo(self, l, j):
        P = self.P
        sl = slice(j * T, (j + 1) * T)
        for half in range(2):
            W, Bw = self.need(l, ("WO", half))
            for q in range(4):
                if P.planning:
                    continue
                m = half * 4 + q
                ps, Bps = self.psring.get()
                self.mm(ps, Bps, [W[:, k, q * 128:(q + 1) * 128] for k in range(KC)],
                        [self.r24[:, 16 + k, :] for k in range(KC)], [Bw] + self.B24[16:24])
                self.tt(self.xs[:, m, sl], self.xs[:, m, sl], ps, ALU.add, [self.Bxs[m][j], Bps], [self.Bxs[m][j]])
            self.release()

    def ffn(self, l, j):
        P = self.P
        sl = slice(j * T, (j + 1) * T)
        for g in range(6):
            nq = 4 if g < 5 else 2
            G, BG = self.g4[g % 2], self.Bg4[g % 2]
            W, Bw = self.need(l, ("FG", g))
            for q in range(nq):
                if P.planning:
                    continue
                ps, Bps = self.proj(W, Bw, q)
                self.act(G[q], ps, AF.Silu, [Bps], [BG[q]])
            self.release()
            W, Bw = self.need(l, ("FU", g))
            for q in range(nq):
                if P.planning:
                    continue
                f = g * 4 + q
                ps, Bps = self.proj(W, Bw, q)
                self.tt(self.r24[:, f, :], G[q], ps, ALU.mult, [BG[q], Bps], [self.B24[f]])
            self.release()
        for cg in range(2):
            banks = [self.psring.get() for _ in range(4)] if not P.planning else None
            for kg in range(3):
                nk = 8 if kg < 2 else 6
                W, Bw = self.need(l, ("FD", cg, kg))
                for q in range(4):
                    if P.planning:
                        continue
                    ps, Bps = banks[q]

                    def fn(e, W=W, q=q, kg=kg, nk=nk, ps=ps):
                        ins = None
                        for k in range(nk):
                            f = kg * 8 + k
                            ins = e.matmul(ps, lhsT=W[:, k, q * 128:(q + 1) * 128], rhs=self.r24[:, f, :],
                                           start=(f == 0), stop=(f == FC - 1))
                        return ins
                    P.op("pe", fn, [Bw] + self.B24[kg * 8:kg * 8 + nk], [Bps])
                self.release()
            for q in range(4):
                if P.planning:
                    continue
                m = cg * 4 + q
                ps, Bps = banks[q]
                self.tt(self.xs[:, m, sl], self.xs[:, m, sl], ps, ALU.add, [self.Bxs[m][j], Bps], [self.Bxs[m][j]])

    def final_norm(self, j):
        P = self.P
        if P.planning:
            return
        sl = slice(j * T, (j + 1) * T)
        sqb = [self.B24[16 + c] for c in range(KC)]
        for c in range(KC):
            self.act(self.r24[:, 16 + c, :], self.xs[:, c, sl], AF.Square, [self.Bxs[c][j]], [sqb[c]])
        ps, Bps = self.psring.get()
        self.mm(ps, Bps, [self.ones[:]] * KC, [self.r24[:, 16 + c, :] for c in range(KC)], [self.Bones] + sqb)
        self.act(self.rstd[:], ps, AF.Sqrt, [Bps], [self.Brstd], scale=1.0 / D, bias=EPS)
        P.op("dve", lambda e: e.reciprocal(out=self.rstd[:], in_=self.rstd[:]), [self.Brstd], [self.Brstd])
        for c in range(KC):
            y, By = self.uring.get()
            self.stt(y, self.xs[:, c, sl], self.vcol(0, "final", c), self.rstd[:], ALU.mult, ALU.mult,
                     [self.Bxs[c][j], self.Brstd, self.Bvec], [By])
            P.dma("sp", self.y_out[:, c, sl], y, reads=[By], semkey="st")

    def exchange(self, idx, src_ap, Bsrc):
        P = self.P
        if P.planning:
            if (not self.fused) and idx >= self.stop:
                self.done = True
            return
        if self.fused:
            P.dma("sp", self.exl[idx], src_ap, reads=[Bsrc], writes=[self.Bexd[idx]], semkey="ex")
            k, v = self.Bexd[idx].lw
            sem = P.sems[k]
            P.streams["pool"].append(lambda e, sem=sem, v=v: e.wait_ge(sem, v))
            P.cnt["ex"] += 16
            ev = ("ex", P.cnt["ex"])
            exs = P.sems["ex"]
            P.streams["pool"].append(lambda e, idx=idx, exs=exs: e.collective_compute(
                "AllGather", ALU.bypass, replica_groups=[list(range(NCORE))],
                ins=[self.exl[idx]], outs=[self.exgd[idx]]).then_inc(exs, 16))
            g = Buf()
            g.lw = ev
            P.dma("sp", self.exg[:], self.exgd[idx].rearrange("(r p) w -> p r w", p=128), reads=[g],
                  writes=[self.Bexg], semkey="ld")
        else:
            if idx < self.stop:
                P.dma("sp", self.exg[:], self.exg_in[idx], writes=[self.Bexg], semkey="ld")
            else:
                P.dma("sp", self.exo, src_ap, reads=[Bsrc], semkey="st")
                self.done = True

    def carry_out(self, l, idx):
        P = self.P
        if P.planning:
            self.exchange(idx, None, None)
            return
        P.op("dve", lambda e: e.tensor_reduce(out=self.car[:, 16:24], in_=self.rs[:], axis=mybir.AxisListType.X,
                                              op=ALU.add), [self.Brs], [self.Bcar])
        self.tt(self.car[:, 16:24], self.car[:, 16:24], self.der[:, l * 16:l * 16 + 8], ALU.mult,
                [self.Bcar, self.Bder], [self.Bcar])
        self.act(self.car[:, 0:8], self.car[:, 16:24], AF.Exp, [self.Bcar], [self.Bcar])
        self.cp(self.car[:, 8:16], self.hst[:], self.Bhst, [self.Bcar])
        self.exchange(idx, self.car[:], self.Bcar)
        if self.done:
            return
        st, t = self.fold[:, 0, :], self.fold[:, 1, :]
        P.op("dve", lambda e: e.memset(st, 0.0), [], [self.Bfold])
        for jj in range(NCORE):
            A_j, h_j = self.exg[:, jj, 0:8], self.exg[:, jj, 8:16]
            self.stt(t, A_j, -1.0, st, ALU.add, ALU.mult, [self.Bexg, self.Bfold], [self.Bfold])
            self.tt(t, t, h_j, ALU.add, [self.Bfold, self.Bexg], [self.Bfold])
            self.stt(st, t, self.msk[:, jj:jj + 1], st, ALU.mult, ALU.add, [self.Bfold, self.Bmsk], [self.Bfold])
        self.cp(self.hst[:], st, [self.Bfold], self.Bhst)

    def halo_exchange(self, idx):
        P = self.P
        if P.planning:
            self.exchange(idx, None, None)
            return
        self.cp(self.car[:].rearrange("p (c w) -> p c w", w=4), self.xs[:, :, NTOK - 4:NTOK],
                [self.Bxs[c][NT - 1] for c in range(KC)], [self.Bcar])
        self.exchange(idx, self.car[:], self.Bcar)
        if self.done:
            return
        xhf = self.xh[:].rearrange("p c w -> p (c w)")
        P.op("dve", lambda e: e.memset(xhf, 0.0), [], [self.Bxh])
        for jj in range(NCORE):
            self.stt(xhf, self.exg[:, jj, :], self.msk[:, 8 + jj:9 + jj], xhf, ALU.mult, ALU.add,
                     [self.Bexg, self.Bmsk, self.Bxh], [self.Bxh])

    def program(self):
        P = self.P
        pl = P.planning
        if not pl:
            self.convert_layer(0)
            if self.fused or self.stop >= 2:
                self.convert_layer(1)
            P.dma("sp", self.vec[:], self.vec_in, writes=[self.Bvec], semkey="ld")
            P.dma("sp", self.msk[:], self.msk_in, writes=[self.Bmsk], semkey="ld")
            P.dma("sp", self.xh[:], self.xh_in, writes=[self.Bxh], semkey="ld")
            for j in range(NT):
                sl = slice(j * T, (j + 1) * T)
                P.dma("sp", self.xs[:, :, sl], self.x_in[:, :, sl], writes=[self.Bxs[c][j] for c in range(KC)],
                      semkey="ld")
            P.op("dve", lambda e: e.memset(self.ones[:], 1.0), [], [self.Bones])
            for l in range(L):
                d1 = self.der[:, l * 16:l * 16 + 8]
                d2 = self.der[:, l * 16 + 8:l * 16 + 16]
                lam = self.vec[:, l * NV + V_LAM:l * NV + V_LAM + 8]
                self.act(d1, lam, AF.Exp, [self.Bvec], [self.Bder], scale=-1.0)
                self.act(d1, d1, AF.Ln, [self.Bder], [self.Bder], bias=1.0)
                P.op("dve", lambda e, d1=d1, d2=d2: e.tensor_scalar(out=d2, in0=d1, scalar1=-16.0, scalar2=None,
                                                                    op0=ALU.mult), [self.Bder], [self.Bder])
                P.op("dve", lambda e, d1=d1: e.tensor_scalar(out=d1, in0=d1, scalar1=-8.0, scalar2=None,
                                                             op0=ALU.mult), [self.Bder], [self.Bder])
        exi = 0
        for l in range(L):
            if not pl:
                off, nk, nc_ = SLAB_OFF[("LRU",)]
                src = self.wsc[l].rearrange("r w -> (r w)")[off:off + 128 * 2048].rearrange("(p n) -> p n", p=128)
                P.dma("sp", self.lru[:, l, :, :].rearrange("p a b -> p (a b)"), src, reads=[self.Bcv[(l, ("LRU",))]],
                      writes=[self.Blru[l]], semkey="ld")
                P.op("dve", lambda e: e.memset(self.hst[:], 0.0), [], self.Bhst)
            for j in range(NT):
                if not pl:
                    self.norm(l, V_LN1, j, halo=(j == 0))
                self.mixer_b(l, j, 1)
            self.carry_out(l, exi)
            exi += 1
            if self.done:
                return
            for j in range(NT):
                if not pl:
                    self.norm(l, V_LN1, j)
                self.mixer_b(l, j, 2)
                self.mixer_a(l, j)
                self.merge(l, j)
                self.wo(l, j)
                if not pl:
                    self.norm(l, V_LN2, j)
                self.ffn(l, j)
                if l == L - 1:
                    self.final_norm(j)
            if l < L - 1:
                self.halo_exchange(exi)
                exi += 1
                if self.done:
                    return

    def finalize_wait(self):
        P = self.P
        P.raw("sp", lambda e: e.wait_ge(P.sems["st"], P.cnt["st"]))


def _feature_major(a):
    n = a.shape[0]
    return np.ascontiguousarray(a.reshape(n, KC, 128).transpose(2, 1, 0))


def _layout_weights(inp):
    w = np.zeros((L, WTOT), np.float32)
    for l in range(L):
        lru = np.zeros((128, 16, 128), np.float32)
        for wi, key in enumerate(("lru_wa", "lru_wx")):
            for c in range(8):
                lru[0:64, wi * 8 + c, 0:64] = inp[key][l, 2 * c]
                lru[64:128, wi * 8 + c, 64:128] = inp[key][l, 2 * c + 1]
        for name, (key, k0, nk, c0, nc_) in SLABS:
            off = SLAB_OFF[name][0]
            if key == "lru":
                blk = lru
            else:
                m = inp[key][l]
                blk = m[k0 * 128:(k0 + nk) * 128, c0:c0 + nc_].reshape(nk, 128, nc_).transpose(1, 0, 2)
            w[l, off:off + 128 * nk * nc_] = blk.reshape(-1)
    return w.reshape(L, WROWS, ROWW)


def _layout_vecs(inp):
    v = np.zeros((128, L * NV + 8), np.float32)

    def put(col, vec1024):
        v[:, col:col + 8] = vec1024.reshape(8, 128).T

    for l in range(L):
        b = l * NV
        put(b + V_LN1, inp["ln1_g"][l])
        for k in range(3):
            put(b + V_CAW + k * 8, inp["conv_a_w"][l, k])
        for k in range(4):
            put(b + V_CBW + k * 8, inp["conv_b_w"][l, k])
        put(b + V_CBB, inp["conv_b_b"][l])
        put(b + V_BA, inp["lru_ba"][l])
        put(b + V_BX, inp["lru_bx"][l])
        put(b + V_LAM, inp["lru_lambda"][l])
        put(b + V_GBA, inp["gate_bias"][l, 0])
        put(b + V_GBB, inp["gate_bias"][l, 1])
        put(b + V_LN2, inp["ln2_g"][l])
    put(L * NV, inp["final_g"])
    return v


_CACHE = {}


def _get_nc(stop, fused):
    key = (stop, fused)
    if key not in _CACHE:
        _CACHE[key] = Kern(stop, fused).build()
    return _CACHE[key]


def kernel(**inp):
    inp = {k: np.asarray(v) for k, v in inp.items()}
    x = inp["x"].astype(np.float32, copy=False)
    wl = _layout_weights(inp)
    vec = _layout_vecs(inp)
    base = []
    for k in range(NCORE):
        b, q = divmod(k, 4)
        s0 = q * NTOK
        xs = _feature_major(x[b, s0:s0 + NTOK])
        xh = np.zeros((4, D), np.float32)
        if q > 0:
            xh[:] = x[b, s0 - 4:s0]
        msk = np.zeros((128, 16), np.float32)
        for jj in range(NCORE):
            if b * 4 <= jj < k:
                msk[:, jj] = 1.0
            if jj == k - 1 and q > 0:
                msk[:, 8 + jj] = 1.0
        base.append(dict(x_in=xs, xh_in=_feature_major(xh), vec_in=vec, msk_in=msk, w_in=wl))
    cores = list(range(NCORE))
    if FUSED:
        nc = _get_nc(3, True)
        res = run_bass_kernel_spmd(nc, base, core_ids=cores)
        outs = [r["y_out"] for r in res.results]
    else:
        gathered = []
        outs = None
        for stop in range(4):
            nc = _get_nc(stop, False)
            maps = []
            for k in range(NCORE):
                m = dict(base[k])
                for i, g in enumerate(gathered):
                    m[f"exg{i}"] = g
                maps.append(m)
            res = run_bass_kernel_spmd(nc, maps, core_ids=cores)
            if stop < 3:
                g = np.stack([np.asarray(r["exo"]) for r in res.results], axis=1)
                gathered.append(np.ascontiguousarray(g))
            else:
                outs = [r["y_out"] for r in res.results]
    y = np.empty((2, 4 * NTOK, D), np.float32)
    for k in range(NCORE):
        b, q = divmod(k, 4)
        o = np.asarray(outs[k])
        y[b, q * NTOK:(q + 1) * NTOK] = o.transpose(2, 1, 0).reshape(NTOK, D)
    return y
```
